# Optimizing a Trainium2 kernel written in Bass

```python
import jax, jax.numpy as jnp
from jax import lax
import numpy as np

D_MODEL = 1024
BATCH = 32
SEQ = 2048
DEPTH = 2

D_A = 1024
N_HEADS = 4
HEAD_DIM = D_A // N_HEADS
QK_CONV = 4
CHUNK = 64
D_B = 1024
DW_K = 31
EPS = 1e-6

SPLIT_SIZES = (D_A, D_A, D_A, D_A, N_HEADS, N_HEADS, D_A,
               2 * D_B, D_B,
               D_MODEL, D_MODEL)
N_PROJ = int(sum(SPLIT_SIZES))
SPLIT_POINTS = tuple(int(s) for s in np.cumsum(SPLIT_SIZES)[:-1])

kernel_name = "hybrid_mlstm_conformer_conv_gated_merge"


def _rmsnorm(x, g):
    xf = x.astype(jnp.float32)
    y = xf * lax.rsqrt(jnp.mean(xf * xf, axis=-1, keepdims=True) + EPS)
    return (y * g.astype(jnp.float32)).astype(x.dtype)


def _layernorm(x, g, b):
    xf = x.astype(jnp.float32)
    mu = jnp.mean(xf, axis=-1, keepdims=True)
    var = jnp.mean(jnp.square(xf - mu), axis=-1, keepdims=True)
    y = (xf - mu) * lax.rsqrt(var + EPS)
    return (y * g.astype(jnp.float32) + b.astype(jnp.float32)).astype(x.dtype)


def _causal_dwconv(x, w, b):
    k, c = w.shape
    y = lax.conv_general_dilated(
        x, w[:, None, :].astype(x.dtype), window_strides=(1,), padding=[(k - 1, 0)],
        dimension_numbers=("NWC", "WIO", "NWC"), feature_group_count=c)
    return y + b.astype(x.dtype)


def _mlstm(q, k, v, i_pre, f_pre):
    b_, s_, h_, dh = q.shape
    nc = s_ // CHUNK

    def to_chunks(t):
        t = t.astype(jnp.float32).reshape(b_, nc, CHUNK, h_, *t.shape[3:])
        return jnp.moveaxis(t, (1, 3), (0, 2))

    qc = to_chunks(q) * (dh ** -0.5)
    kc, vc = to_chunks(k), to_chunks(v)
    lic = to_chunks(i_pre)
    lfc = to_chunks(jax.nn.log_sigmoid(f_pre.astype(jnp.float32)))
    mask = jnp.tril(jnp.ones((CHUNK, CHUNK), dtype=bool))

    def step(carry, inp):
        c_st, n_st, m_st = carry
        qx, kx, vx, li, lf = inp
        bcum = jnp.cumsum(lf, axis=-1)
        dmat = jnp.where(mask, bcum[..., :, None] - bcum[..., None, :] + li[..., None, :], -jnp.inf)
        m_inter = bcum + m_st[..., None]
        m_t = jnp.maximum(m_inter, jnp.max(dmat, axis=-1))
        scores = jnp.einsum("bhtd,bhsd->bhts", qx, kx) * jnp.exp(dmat - m_t[..., None])
        a_inter = jnp.exp(m_inter - m_t)
        num = jnp.einsum("bhts,bhse->bhte", scores, vx) + \
            a_inter[..., None] * jnp.einsum("bhtd,bhde->bhte", qx, c_st)
        den = jnp.sum(scores, axis=-1) + a_inter * jnp.einsum("bhtd,bhd->bht", qx, n_st)
        h = num / jnp.maximum(jnp.abs(den), jnp.exp(-m_t))[..., None]
        b_last = bcum[..., -1]
        g = b_last[..., None] - bcum + li
        m_new = jnp.maximum(b_last + m_st, jnp.max(g, axis=-1))
        wk = jnp.exp(g - m_new[..., None])
        a_c = jnp.exp(b_last + m_st - m_new)
        c_new = a_c[..., None, None] * c_st + jnp.einsum("bhs,bhsd,bhse->bhde", wk, kx, vx)
        n_new = a_c[..., None] * n_st + jnp.einsum("bhs,bhsd->bhd", wk, kx)
        return (c_new, n_new, m_new), h

    init = (jnp.zeros((b_, h_, dh, dh), jnp.float32),
            jnp.zeros((b_, h_, dh), jnp.float32),
            jnp.zeros((b_, h_), jnp.float32))
    _, hs = lax.scan(step, init, (qc, kc, vc, lic, lfc))
    return jnp.moveaxis(hs, (0, 2), (1, 3)).reshape(b_, s_, h_, dh)


def _layer(x, norm_g, w_in, b_if, conv_qk_w, conv_qk_b, mhn_g, dw_w, dw_b, ln_g, ln_b, w_pa, w_pb, w_out):
    b_, s_, _ = x.shape
    h = _rmsnorm(x, norm_g)
    p = jnp.einsum("bsd,dn->bsn", h, w_in.astype(h.dtype))
    q, k, v, o_pre, i_pre, f_pre, z_a, glu_in, z_b, ga_pre, gb_pre = jnp.split(p, SPLIT_POINTS, axis=-1)

    qk = jax.nn.silu(_causal_dwconv(jnp.concatenate([q, k], axis=-1), conv_qk_w, conv_qk_b))
    q, k = qk[..., :D_A], qk[..., D_A:]
    i_pre = i_pre + b_if[:N_HEADS].astype(p.dtype)
    f_pre = f_pre + b_if[N_HEADS:].astype(p.dtype)
    shp = (b_, s_, N_HEADS, HEAD_DIM)
    hm = _mlstm(q.reshape(shp), k.reshape(shp), v.reshape(shp), i_pre, f_pre)
    hm = hm * lax.rsqrt(jnp.mean(hm * hm, axis=-1, keepdims=True) + EPS)
    hm = hm.reshape(b_, s_, D_A) * mhn_g.astype(jnp.float32)
    out_a = (jax.nn.sigmoid(o_pre.astype(jnp.float32)) * hm).astype(x.dtype) * jax.nn.silu(z_a)
    y_a = jnp.einsum("bsc,cd->bsd", out_a, w_pa.astype(out_a.dtype))

    u = glu_in[..., :D_B] * jax.nn.sigmoid(glu_in[..., D_B:])
    u = _causal_dwconv(u, dw_w, dw_b)
    u = jax.nn.silu(_layernorm(u, ln_g, ln_b)) * jax.nn.silu(z_b)
    y_b = jnp.einsum("bsc,cd->bsd", u, w_pb.astype(u.dtype))

    merged = jax.nn.sigmoid(ga_pre) * y_a + jax.nn.sigmoid(gb_pre) * y_b
    return x + jnp.einsum("bsd,de->bse", merged, w_out.astype(merged.dtype))


def setup_inputs(seed: int = 0) -> dict:
    key = jax.random.key(seed)
    ks = jax.random.split(key, 16)
    L = DEPTH
    nrm = lambda k, shape, scale: jax.random.normal(k, shape, jnp.float32) * scale
    f_bias = jnp.tile(jnp.linspace(3.0, 6.0, N_HEADS, dtype=jnp.float32), (L, 1))
    b_if = jnp.concatenate([nrm(ks[3], (L, N_HEADS), 0.1),
                            f_bias + nrm(ks[4], (L, N_HEADS), 0.1)], axis=-1)
    return {
        "x": nrm(ks[0], (BATCH, SEQ, D_MODEL), 1.0),
        "norm_g": 1.0 + nrm(ks[1], (L, D_MODEL), 0.02),
        "w_in": nrm(ks[2], (L, D_MODEL, N_PROJ), D_MODEL ** -0.5),
        "b_if": b_if,
        "conv_qk_w": nrm(ks[5], (L, QK_CONV, 2 * D_A), QK_CONV ** -0.5),
        "conv_qk_b": nrm(ks[6], (L, 2 * D_A), 0.02),
        "mhn_g": 1.0 + nrm(ks[7], (L, D_A), 0.02),
        "dw_w": nrm(ks[8], (L, DW_K, D_B), DW_K ** -0.5),
        "dw_b": nrm(ks[9], (L, D_B), 0.02),
        "ln_g": 1.0 + nrm(ks[10], (L, D_B), 0.02),
        "ln_b": nrm(ks[11], (L, D_B), 0.02),
        "w_pa": nrm(ks[12], (L, D_A, D_MODEL), D_A ** -0.5),
        "w_pb": nrm(ks[13], (L, D_B, D_MODEL), D_B ** -0.5),
        "w_out": nrm(ks[14], (L, D_MODEL, D_MODEL), D_MODEL ** -0.5),
        "final_g": 1.0 + nrm(ks[15], (D_MODEL,), 0.02),
    }


def reference(x, norm_g, w_in, b_if, conv_qk_w, conv_qk_b, mhn_g, dw_w, dw_b, ln_g, ln_b,
              w_pa, w_pb, w_out, final_g):
    for l in range(DEPTH):
        x = _layer(x, norm_g[l], w_in[l], b_if[l], conv_qk_w[l], conv_qk_b[l], mhn_g[l],
                   dw_w[l], dw_b[l], ln_g[l], ln_b[l], w_pa[l], w_pb[l], w_out[l])
    return _rmsnorm(x, final_g)
```

```python
import numpy as np
import concourse.bass as bass
import concourse.mybir as mybir
from concourse.bass_utils import run_bass_kernel_spmd

F32 = mybir.dt.float32
BF16 = mybir.dt.bfloat16
AF = mybir.ActivationFunctionType
ALU = mybir.AluOpType

D = 1024
NPROJ = 10248
SEQ = 2048
TT = 512
NB = 4
L = 2
EPS = 1e-6
C_Q, C_K, C_V, C_O, C_IF, C_ZA, C_GLA, C_GLB, C_ZB, C_GA, C_GB = (
    0, 1024, 2048, 3072, 4096, 4104, 5128, 6152, 7176, 8200, 9224)

LC_NG, LC_QKW, LC_QKB, LC_MG, LC_DWW, LC_DWB, LC_LNG, LC_LNB, LC_BIF, LC_WIF = (
    0, 8, 72, 88, 96, 352, 360, 368, 376, 408)
LC_SIZE = 472
GC_FG = L * LC_SIZE
GC_TRI = GC_FG + 1024
GC_ID = GC_TRI + 128
GC_IDR = GC_ID + 128
NCONST = GC_IDR + 32

SLOT = 4352


class Sched:
    ENG = ("tensor", "scalar", "vector", "gpsimd", "sync")

    def __init__(self, nc):
        self.nc = nc
        self.q = {e: [] for e in self.ENG}
        self.sems = {}
        self.cnt = {}
        self.waited = {e: {} for e in self.ENG}
        self.rec = {}
        self.ninst = 0
        self.nops = {e: 0 for e in self.ENG}
        self.marks = []
        self.dry = False

    def add_sem(self, key, sem):
        self.sems[key] = sem
        self.cnt[key] = 0

    def ap_range(self, ap):
        name = ap.name
        dims = ap.ap
        pstep = dims[0][0]
        off = ap.offset % pstep if pstep > 0 else ap.offset
        ext = 1
        for st, c in dims[1:]:
            ext += abs(st) * (c - 1)
        es = DT_SIZE[ap.dtype]
        lo = off * es
        hi = (off + ext) * es
        if str(ap.space) == "PSUM" or "PSUM" in str(ap.space):
            lo = (lo // 2048) * 2048
            hi = ((hi + 2047) // 2048) * 2048
        return name, lo, hi

    def _deps(self, eng, reads, writes):
        deps = {}

        def need(tk):
            if tk is None:
                return
            k, v = tk
            assert v <= self.cnt[k], ("dependency on a not-yet-emitted signal", k, v, self.cnt[k])
            if deps.get(k, 0) < v:
                deps[k] = v

        for ap in reads:
            name, lo, hi = self.ap_range(ap)
            is_ps = "PSUM" in str(ap.space)
            for r in self.rec.get(name, []):
                if r[0] < hi and lo < r[1]:
                    need(r[2])
                    if is_ps:
                        for k, v in r[3].items():
                            if k != eng:
                                need((k, v))
        for ap in writes:
            name, lo, hi = self.ap_range(ap)
            for r in self.rec.get(name, []):
                if r[0] < hi and lo < r[1]:
                    w = r[2]
                    if w is not None and not (w[0] == eng and eng in SAME_ENG_OK):
                        need(w)
                    for k, v in r[3].items():
                        if not (k == eng and eng in SAME_ENG_OK):
                            need((k, v))
        return deps

    def _update(self, reads, writes, tk):
        for ap in reads:
            name, lo, hi = self.ap_range(ap)
            lst = self.rec.setdefault(name, [])
            new = []
            covered = []
            for r in lst:
                if r[0] < hi and lo < r[1]:
                    if r[0] < lo:
                        new.append([r[0], lo, r[2], dict(r[3])])
                    if hi < r[1]:
                        new.append([hi, r[1], r[2], dict(r[3])])
                    a, b = max(r[0], lo), min(r[1], hi)
                    rd = dict(r[3])
                    if rd.get(tk[0], 0) < tk[1]:
                        rd[tk[0]] = tk[1]
                    new.append([a, b, r[2], rd])
                    covered.append((a, b))
                else:
                    new.append(r)
            covered.sort()
            cur = lo
            for a, b in covered:
                if a > cur:
                    new.append([cur, a, None, {tk[0]: tk[1]}])
                cur = max(cur, b)
            if cur < hi:
                new.append([cur, hi, None, {tk[0]: tk[1]}])
            self.rec[name] = new
        for ap in writes:
            name, lo, hi = self.ap_range(ap)
            lst = self.rec.setdefault(name, [])
            new = []
            for r in lst:
                if r[0] < hi and lo < r[1]:
                    if r[0] < lo:
                        new.append([r[0], lo, r[2], dict(r[3])])
                    if hi < r[1]:
                        new.append([hi, r[1], r[2], dict(r[3])])
                else:
                    new.append(r)
            new.append([lo, hi, tk, {}])
            self.rec[name] = new

    def _emit_waits(self, eng, deps):
        for k, v in deps.items():
            if self.waited[eng].get(k, 0) >= v:
                continue
            self.waited[eng][k] = v
            sem = self.sems[k]
            self.q[eng].append(lambda e, sem=sem, v=v: e.wait_ge(sem, v))
            self.ninst += 1

    def op(self, eng, fn, reads=(), writes=(), signal=True):
        if self.dry:
            return None
        deps = self._deps(eng, reads, writes)
        self._emit_waits(eng, deps)
        tk = (eng, self.cnt[eng] + 1)
        if signal:
            self.cnt[eng] += 1
            sem = self.sems[eng]
            self.q[eng].append(lambda e, fn=fn, sem=sem: fn(e).then_inc(sem, 1))
        else:
            self.q[eng].append(lambda e, fn=fn: fn(e))
        self.ninst += 1
        self.nops[eng] += 1
        self._update(reads, writes, tk)
        return tk

    def mark(self, label):
        if self.dry:
            return
        self.marks.append((label, dict(self.nops)))

    def dma(self, eng, semkey, out, in_, track_out=True, track_in=True, serialize=True, **kw):
        if self.dry:
            return None
        onchip = lambda a: str(a.space) in ("SB", "PSUM")
        reads = [in_] if (track_in and onchip(in_)) else []
        writes = [out] if (track_out and onchip(out)) else []
        deps = self._deps("dma", reads, writes)
        if serialize and self.cnt[semkey] > 0:
            deps[semkey] = max(deps.get(semkey, 0), self.cnt[semkey])
        self._emit_waits(eng, deps)
        self.cnt[semkey] += 16
        tk = (semkey, self.cnt[semkey])
        sem = self.sems[semkey]
        self.q[eng].append(lambda e, out=out, in_=in_, sem=sem, kw=kw: e.dma_start(out=out, in_=in_, **kw).then_inc(sem, 16))
        self.ninst += 1
        self._update(reads, writes, tk)
        return tk

    def dma_group(self, eng, semkey, pairs):
        if self.dry:
            return None
        reads = [i for _, i in pairs]
        writes = [o for o, _ in pairs]
        deps = self._deps("dma", reads, writes)
        if self.cnt[semkey] > 0:
            deps[semkey] = max(deps.get(semkey, 0), self.cnt[semkey])
        self._emit_waits(eng, deps)
        sem = self.sems[semkey]
        for out, in_ in pairs:
            self.cnt[semkey] += 16
            self.q[eng].append(lambda e, out=out, in_=in_, sem=sem: e.dma_start(out=out, in_=in_).then_inc(sem, 16))
            self.ninst += 1
        tk = (semkey, self.cnt[semkey])
        self._update(reads, writes, tk)
        return tk

    def wait_all(self, eng, keys):
        if self.dry:
            return
        deps = {k: self.cnt[k] for k in keys if self.cnt[k] > 0}
        self._emit_waits(eng, deps)


DT_SIZE = {F32: 4, BF16: 2}
SAME_ENG_OK = ("tensor",)


class Banks:
    def __init__(self, banks):
        self.banks = banks
        self.free = list(range(len(banks)))

    def get(self):
        assert self.free, "out of PSUM banks"
        i = self.free.pop(0)
        return i

    def rel(self, i):
        assert i not in self.free
        self.free.append(i)

    def reset(self):
        self.free = list(range(len(self.banks)))


def build_program(n_seq, n_tiles, prepass_pieces=1024, debug=None):
    nc = bass.Bass("TRN2", target_bir_lowering=False)
    ntok = n_seq * n_tiles * TT
    x_d = nc.dram_tensor("x", [ntok, D], F32, kind="ExternalInput").ap()
    y_d = nc.dram_tensor("y", [ntok, D], F32, kind="ExternalOutput").ap()
    win_d = nc.dram_tensor("w_in", [L, D, NPROJ], F32, kind="ExternalInput").ap()
    wpa_d = nc.dram_tensor("w_pa", [L, D, D], F32, kind="ExternalInput").ap()
    wpb_d = nc.dram_tensor("w_pb", [L, D, D], F32, kind="ExternalInput").ap()
    wout_d = nc.dram_tensor("w_out", [L, D, D], F32, kind="ExternalInput").ap()
    cp_d = nc.dram_tensor("cpack", [128, NCONST], F32, kind="ExternalInput").ap()
    s_in = nc.dram_tensor("s_in", [L, D, NPROJ], BF16, kind="Internal").ap()
    s_pa = nc.dram_tensor("s_pa", [L, D, D], BF16, kind="Internal").ap()
    s_pb = nc.dram_tensor("s_pb", [L, D, D], BF16, kind="Internal").ap()
    s_out = nc.dram_tensor("s_out", [L, D, D], BF16, kind="Internal").ap()
    s_dg = nc.dram_tensor("s_dg", [L, 4, 128, 4096], BF16, kind="Internal").ap()

    import contextlib
    es = contextlib.ExitStack()
    with es:
        def sb(name, shape, dt):
            return es.enter_context(nc.sbuf_tensor(name, shape, dt))

        def sem(name):
            return es.enter_context(nc.semaphore(name))

        xb = [sb(f"xb{i}", [128, NB, D], F32) for i in range(2)]
        arena = sb("arena", [128, 9 * SLOT], BF16)
        NW = 3
        wbuf = [sb(f"wbuf{i}", [128, 8, 512], BF16) for i in range(NW)]
        Cst = [sb(f"Cst{l}", [128, 2, 4, 257], F32) for l in range(L)]
        Cbf = [sb(f"Cbf{l}", [128, 2, 4, 257], BF16) for l in range(L)]
        cp = sb("cp", [128, NCONST], F32)
        ident = sb("ident", [128, 128], BF16)
        identr = sb("identr", [128, 32], BF16)
        onesb = sb("onesb", [128, 128], BF16)
        ones32 = sb("ones32", [128, 128], F32)
        wif = sb("wif", [128, L, 8, 8], BF16)
        hist_u = [sb(f"hist_u{l}", [128, 8, 30], BF16) for l in range(L)]
        hist_qk = [sb(f"hist_qk{l}", [128, 16, 3], BF16) for l in range(L)]
        NTMP = 4
        tmpf = [sb(f"tmpf{i}", [128, 512], F32) for i in range(NTMP)]
        NTB = 4
        tmpb = [sb(f"tmpb{i}", [128, 512], BF16) for i in range(NTB)]
        junk = sb("junk", [128, 1024], BF16)
        small = sb("small", [128, 256], F32)
        S0T = [sb(f"S0T{i}", [128, 4, 128], BF16) for i in range(2)]
        H1 = [sb(f"H1_{i}", [128, 4, 256], F32) for i in range(2)]
        HM = [sb(f"HM{i}", [128, 4, 256], BF16) for i in range(2)]

        psb = [es.enter_context(nc.psum_tensor(f"ps{i}", [128, 512], F32)) for i in range(8)]
        banks = Banks(psb)

        S = Sched(nc)
        for e in ("tensor", "scalar", "vector", "gpsimd"):
            S.add_sem(e, sem("s_" + e))
        for i in range(NW):
            S.add_sem(f"w{i}", sem(f"s_w{i}"))
        for i in range(2):
            S.add_sem(f"x{i}", sem(f"s_x{i}"))
            S.add_sem(f"o{i}", sem(f"s_o{i}"))
        S.add_sem("cast0", sem("s_cast0"))
        S.add_sem("cast1", sem("s_cast1"))
        NBB = 8
        for i in range(NBB):
            S.add_sem(f"bb{i}", sem(f"s_bb{i}"))
        S.add_sem("dgst", sem("s_dgst"))
        S.add_sem("dbg", sem("s_dbg"))
        dbg_done = set()

        def dbg(name, ap):
            if debug is None or name in dbg_done or S.dry:
                return
            dbg_done.add(name)
            shp = list(ap.shape)
            dt_ = nc.dram_tensor("dbg_" + name, shp, ap.dtype, kind="ExternalOutput").ap()
            debug[name] = shp
            S.dma(P, "dbg", dt_, ap, track_out=False)
        S.add_sem("const", sem("s_const"))

        def av(slot0, nelem_bf16, dt=BF16, off=0):
            a = arena[:, slot0 * SLOT + off: slot0 * SLOT + off + nelem_bf16]
            if dt == F32:
                a = a.bitcast(F32)
            return a

        Hh = av(0, 4096).rearrange("p (c t) -> p c t", c=8)
        XN = av(1, 4096).rearrange("p (b d) -> p b d", b=4)
        Ue = av(1, 8 * 543).rearrange("p (c t) -> p c t", c=8)
        BPK = av(5, 8 * 4 * 540).rearrange("p (j g m) -> p j g m", j=8, g=4)
        UC = av(2, 8192, F32).rearrange("p (c t) -> p c t", c=8)
        ZB = av(4, 4096).rearrange("p (c t) -> p c t", c=8)
        UB = av(5, 4096).rearrange("p (c t) -> p c t", c=8)
        LNS = av(6, 4096, F32).rearrange("p (m t) -> p m t", m=4)
        QKe = av(7, 16 * 515).rearrange("p (c t) -> p c t", c=16)
        Qf = av(1, 4096).rearrange("p (c t) -> p c t", c=8)
        Kf = av(2, 4096).rearrange("p (c t) -> p c t", c=8)
        GA = av(3, 4096).rearrange("p (c t) -> p c t", c=8)
        Vt = av(4, 4 * 4 * 257).rearrange("p (b h e) -> p b h e", b=4, h=4)
        KPP = av(6, 4096).rearrange("p (b d) -> p b d", b=4)
        OA = av(7, 4096).rearrange("p (c t) -> p c t", c=8)
        MG = av(3, 4096).rearrange("p (c t) -> p c t", c=8)
        OUT = av(7, 8192, F32).rearrange("p (b d) -> p b d", b=4)
        SG2 = av(2, 4096, F32).rearrange("p (m t) -> p m t", m=4)
        YB = av(8, 4096).rearrange("p (c t) -> p c t", c=8)
        SG3 = av(8, 4096, F32).rearrange("p (m t) -> p m t", m=4)
        SGA = av(1, 4096, F32).rearrange("p (m t) -> p m t", m=4)
        SGB = av(2, 4096, F32).rearrange("p (m t) -> p m t", m=4)

        def sm(lo, n):
            return small[:, lo:lo + n]
        ssq = sm(0, 4)
        lnv = sm(4, 4)
        rstd = sm(8, 4)
        gif = sm(16, 32)
        egf = sm(48, 16)
        nlf = sm(64, 16)
        r16 = sm(80, 16)
        ebt = sm(96, 16)
        a1 = sm(112, 16)
        csc = sm(128, 16)
        a2 = sm(144, 16)
        c2 = sm(160, 16)
        dn = sm(176, 4)
        rec = sm(180, 4)
        hs = sm(184, 4)
        ssq2 = sm(188, 4)
        lnv2 = sm(192, 4)
        rs2 = sm(196, 4)
        hs2 = sm(200, 4)

        def lc(l, off, n):
            return cp[:, l * LC_SIZE + off: l * LC_SIZE + off + n]

        V, A, P, T = "vector", "scalar", "gpsimd", "tensor"

        def act(out, in_, func, bias=None, scale=None, accum_out=None, extra_reads=()):
            kw = {}
            reads = [in_] + list(extra_reads)
            if bias is not None:
                kw["bias"] = bias
                if not isinstance(bias, (int, float)):
                    reads.append(bias)
            if scale is not None:
                kw["scale"] = scale
                if not isinstance(scale, (int, float)):
                    reads.append(scale)
            writes = [out]
            if accum_out is not None:
                kw["accum_out"] = accum_out
                writes.append(accum_out)
            return S.op(A, lambda e: e.activation(out=out, in_=in_, func=func, **kw), reads, writes)

        def tt(eng, out, in0, in1, op):
            return S.op(eng, lambda e: e.tensor_tensor(out=out, in0=in0, in1=in1, op=op), [in0, in1], [out])

        def ts(eng, out, in0, s1, op0, s2=None, op1=None):
            reads = [in0]
            if not isinstance(s1, (int, float)):
                reads.append(s1)
            if s2 is not None and not isinstance(s2, (int, float)):
                reads.append(s2)
            if op1 is None:
                return S.op(eng, lambda e: e.tensor_scalar(out=out, in0=in0, scalar1=s1, scalar2=None, op0=op0), reads, [out])
            return S.op(eng, lambda e: e.tensor_scalar(out=out, in0=in0, scalar1=s1, scalar2=s2, op0=op0, op1=op1), reads, [out])

        def stt(eng, out, in0, scalar, in1, op0, op1):
            reads = [in0, in1]
            if not isinstance(scalar, (int, float)):
                reads.append(scalar)
            return S.op(eng, lambda e: e.scalar_tensor_tensor(out=out, in0=in0, scalar=scalar, in1=in1, op0=op0, op1=op1), reads, [out])

        def cpy(eng, out, in_):
            return S.op(eng, lambda e: e.tensor_copy(out=out, in_=in_), [in_], [out])

        def mset(eng, out, val):
            return S.op(eng, lambda e: e.memset(out, val), [], [out])

        def mm(out, lhsT, rhs, start, stop, signal=None, tile_position=None):
            if signal is None:
                signal = stop
            if tile_position is not None:
                return S.op(T, lambda e: e.matmul(out, lhsT=lhsT, rhs=rhs, start=start, stop=stop, tile_position=tile_position),
                            [lhsT, rhs], [out], signal=signal)
            return S.op(T, lambda e: e.matmul(out, lhsT=lhsT, rhs=rhs, start=start, stop=stop), [lhsT, rhs], [out], signal=signal)

        def tr(out, in_, signal):
            return S.op(T, lambda e: e.transpose(out, in_, ident[:]), [in_, ident[:]], [out], signal=signal)

        def pbank():
            i = banks.get()
            return i, psb[i]

        tmpf_i = [0]
        def tf():
            tmpf_i[0] = (tmpf_i[0] + 1) % NTMP
            return tmpf[tmpf_i[0]]
        tmpb_i = [0]
        def tb():
            tmpb_i[0] = (tmpb_i[0] + 1) % NTB
            return tmpb[tmpb_i[0]]
        for l in range(L):
            c0 = 0
            while c0 < NPROJ:
                c1 = min(NPROJ, c0 + prepass_pieces)
                S.dma(P, f"cast{l}", s_in[l, :, c0:c1], win_d[l, :, c0:c1], track_out=False, track_in=False, serialize=False)
                c0 = c1
            for sd, wd in ((s_pa, wpa_d), (s_pb, wpb_d), (s_out, wout_d)):
                S.dma(P, f"cast{l}", sd[l], wd[l], track_out=False, track_in=False, serialize=False)
        S.dma("sync", "const", cp[:], cp_d, serialize=False)
        cpy(V, ident[:], cp[:, GC_ID:GC_ID + 128])
        cpy(V, identr[:], cp[:, GC_IDR:GC_IDR + 32])
        mset(V, onesb[:], 1.0 / 1024.0)
        mset(V, ones32[:], 1.0)
        mset(V, small[:], 0.0)
        for l in range(L):
            cpy(V, wif[:, l, :, :], lc(l, LC_WIF, 64).rearrange("p (k n) -> p k n", k=8))
        tri = cp[:, GC_TRI:GC_TRI + 128]
        fg = cp[:, GC_FG:GC_FG + 1024]
        def dg_ncols(idx):
            return 32 * 128
        k = 0
        for l in range(L):
            for idx in range(4):
                stag = wbuf[k % NW][:].rearrange("p k n -> p (k n)")
                stsem = f"w{k % NW}"
                k += 1
                if idx < 2:
                    for kk in range(128):
                        jj, pg = kk // 32, kk % 32
                        ts(V, stag[:, kk * 32:(kk + 1) * 32], identr[:], lc(l, LC_DWW + (idx * 4 + jj) * 32 + pg, 1), ALU.mult)
                else:
                    for jj in range(8):
                        j = (idx - 2) * 8 + jj
                        for tap in range(4):
                            sl = jj * 4 + tap
                            ts(V, stag[:, sl * 128:(sl + 1) * 128], ident[:], lc(l, LC_QKW + j * 4 + tap, 1), ALU.mult)
                n = dg_ncols(idx)
                S.dma("sync", stsem, s_dg[l, idx, :, 0:n], stag[:, 0:n])

        wlist = []
        wstate = {"loaded": 0, "use": 0}
        PF = NW - 1

        def emit_wload(i):
            kind, l, c0 = wlist[i]
            slot = i % NW
            if l == 1 and not wstate.get("cast1_waited"):
                wstate["cast1_waited"] = True
                S.wait_all("sync", ["cast1"])
            if kind == "dg":
                n = dg_ncols(c0)
                S.dma("sync", f"w{slot}", wbuf[slot][:].rearrange("p k n -> p (k n)")[:, 0:n], s_dg[l, c0, :, 0:n], track_in=False)
                return
            src = {"in": s_in, "pa": s_pa, "pb": s_pb, "out": s_out}[kind]
            srcap = src[l, :, c0:c0 + 512].rearrange("(k p) n -> p k n", p=128)
            S.dma("sync", f"w{slot}", wbuf[slot][:], srcap, track_in=False)

        def next_w(kind_, l_, c0_):
            if S.dry:
                wlist.append((kind_, l_, c0_))
                return wbuf[0]
            i = wstate["use"]
            kind, l, c0 = wlist[i]
            assert (kind, l, c0) == (kind_, l_, c0_), (kind, l, c0, kind_, l_, c0_)
            while wstate["loaded"] <= min(i + PF, len(wlist) - 1):
                emit_wload(wstate["loaded"])
                wstate["loaded"] += 1
            wstate["use"] += 1
            return wbuf[i % NW]


        tiles = [(s, t) for s in range(n_seq) for t in range(n_tiles)]

        def emit_xload(ti):
            s, t = tiles[ti]
            r0 = (s * n_tiles + t) * TT
            S.dma("sync", f"x{ti % 2}", xb[ti % 2][:], x_d[r0:r0 + TT, :].rearrange("(b p) d -> p b d", p=128))


        def feat_proj(w, m, evac):
            bi, ps = pbank()
            for kc in range(8):
                mm(ps[:], w[:, kc, m * 128:(m + 1) * 128], Hh[:, kc, :], kc == 0, kc == 7)
            evac(ps)
            banks.rel(bi)

        def rms_stats(x):
            for b in range(NB):
                act(junk[:], x[:, b, :], AF.Square, accum_out=ssq[:, b:b + 1])
            act(lnv, ssq, AF.Ln, bias=EPS, scale=1.0 / D)
            act(rstd, lnv, AF.Exp, scale=-0.5)

        def main_loop():
            for ti, (s, t) in enumerate(tiles):
                x = xb[ti % 2]
                if t == 0:
                    for l in range(L):
                        mset(P, hist_u[l][:], 0.0)
                        mset(P, hist_qk[l][:], 0.0)
                        mset(P, Cst[l][:], 0.0)
                        mset(P, Cbf[l][:], 0.0)
                for l in range(L):
                    S.mark(f"{ti}.{l}.p0")
                    rms_stats(x)
                    for b in range(NB):
                        if b % 2 == 0:
                            ts(V, XN[:, b, :], x[:, b, :], rstd[:, b:b + 1], ALU.mult)
                        else:
                            act(XN[:, b, :], x[:, b, :], AF.Copy, scale=rstd[:, b:b + 1])
                    for j in range(8):
                        bi, ps = pbank()
                        pst = ps[:].bitcast(BF16)
                        for b in range(NB):
                            tr(pst[:, b * 128:(b + 1) * 128], XN[:, b, j * 128:(j + 1) * 128], signal=(b == NB - 1))
                        act(Hh[:, j, :], pst[:, 0:512], AF.Copy, scale=lc(l, LC_NG + j, 1))
                        banks.rel(bi)
                    dbg("hT", Hh[:])
                    if l == 1 and ti + 1 < len(tiles):
                        emit_xload(ti + 1)

                    S.mark(f"{ti}.{l}.glu")
                    cpy(P, Ue[:, :, 0:30], hist_u[l][:])
                    mset(V, Ue[:, :, 542:543], 0.0)
                    for half in range(2):
                        wb = next_w("in", l, C_GLB + half * 512)
                        for m in range(4):
                            feat_proj(wb, m, lambda ps, m=m: act(SG2[:, m, :], ps[:], AF.Sigmoid))
                        wa = next_w("in", l, C_GLA + half * 512)
                        for m in range(4):
                            j = half * 4 + m
                            feat_proj(wa, m, lambda ps, m=m, j=j: tt(V, Ue[:, j, 30:542], ps[:], SG2[:, m, :], ALU.mult))
                        S.dma_group("sync", f"bb{half % NBB}",
                                    [(BPK[32 * i:32 * i + 32, 4 * half:4 * half + 4, g, :],
                                      Ue[32 * g:32 * g + 32, 4 * half:4 * half + 4, i:i + 540])
                                     for g in range(4) for i in range(4)])
                    cpy(P, hist_u[l][:], Ue[:, :, 512:542])
                    dbg("Ue", Ue[:])
                    S.mark(f"{ti}.{l}.zb")
                    for half in range(2):
                        wz = next_w("in", l, C_ZB + half * 512)
                        for m in range(4):
                            j = half * 4 + m
                            feat_proj(wz, m, lambda ps, j=j: act(ZB[:, j, :], ps[:], AF.Silu))
                    S.mark(f"{ti}.{l}.dwconv")
                    bm, psm = pbank()
                    bq, psq = pbank()
                    pend = None
                    for j in range(8):
                        bi, ps = pbank()
                        if j % 4 == 0:
                            dgt = next_w("dg", l, j // 4)[:].rearrange("p k n -> p (k n)")
                        for p in range(8):
                            for g in range(4):
                                kk = ((j % 4) * 8 + p) * 4 + g
                                mm(ps[32 * g:32 * g + 32, :], dgt[:, kk * 32:(kk + 1) * 32], BPK[:, j, g, 4 * p:4 * p + 512],
                                   p == 0, p == 7, signal=(p == 7 and g == 3), tile_position=(0, 32 * g))
                        act(UC[:, j, :], ps[:], AF.Identity, bias=lc(l, LC_DWB + j, 1))
                        ucb = tb()
                        ucs = tb()
                        act(ucb[:], ps[:], AF.Identity, bias=lc(l, LC_DWB + j, 1))
                        act(ucs[:], ps[:], AF.Square, bias=lc(l, LC_DWB + j, 1))
                        banks.rel(bi)
                        if pend is not None:
                            pb_, ps_, pj = pend
                            mm(psm[:], onesb[:], pb_[:], pj == 0, False)
                            mm(psq[:], onesb[:], ps_[:], pj == 0, False)
                        pend = (ucb, ucs, j)
                    pb_, ps_, pj = pend
                    mm(psm[:], onesb[:], pb_[:], False, True)
                    mm(psq[:], onesb[:], ps_[:], False, True)
                    S.mark(f"{ti}.{l}.ln")
                    mean_sb = LNS[:, 0, :]; msq = LNS[:, 1, :]; var = LNS[:, 1, :]; rstdL = LNS[:, 2, :]; nmr = LNS[:, 3, :]
                    act(mean_sb, psm[:], AF.Identity)
                    act(msq, psm[:], AF.Square)
                    tt(V, var, psq[:], msq, ALU.subtract)
                    banks.rel(bm); banks.rel(bq)
                    act(var, var, AF.Ln, bias=EPS)
                    act(rstdL, var, AF.Exp, scale=-0.5)
                    stt(V, nmr, mean_sb, -1.0, rstdL, ALU.mult, ALU.mult)
                    def ln_chunk(j, l=l, rstdL=rstdL, nmr=nmr):
                        t1 = tf()
                        tt(V, t1[:], UC[:, j, :], rstdL, ALU.mult)
                        tt(P, t1[:], t1[:], nmr, ALU.add)
                        act(t1[:], t1[:], AF.Silu, bias=lc(l, LC_LNB + j, 1), scale=lc(l, LC_LNG + j, 1))
                        tt(V, UB[:, j, :], t1[:], ZB[:, j, :], ALU.mult)

                    S.mark(f"{ti}.{l}.qkproj")
                    cpy(P, QKe[:, :, 0:3], hist_qk[l][:])
                    for qk in range(2):
                        for half in range(2):
                            w = next_w("in", l, (C_Q if qk == 0 else C_K) + half * 512)
                            for m in range(4):
                                j = qk * 8 + half * 4 + m
                                feat_proj(w, m, lambda ps, j=j: cpy(V if j % 2 == 0 else A, QKe[:, j, 3:515], ps[:]) if j % 2 == 0
                                          else act(QKe[:, j, 3:515], ps[:], AF.Copy))
                                if j % 2 == 1:
                                    ln_chunk(j // 2)
                    dbg("UC", UC[:]); dbg("ZB", ZB[:]); dbg("UB", UB[:])
                    cpy(P, hist_qk[l][:], QKe[:, :, 512:515])
                    S.mark(f"{ti}.{l}.qkconv")
                    for j in range(16):
                        if j % 8 == 0:
                            dgt = next_w("dg", l, 2 + j // 8)[:].rearrange("p k n -> p (k n)")
                        bi, ps = pbank()
                        for tap in range(4):
                            sl = (j % 8) * 4 + tap
                            mm(ps[:], dgt[:, sl * 128:(sl + 1) * 128], QKe[:, j, tap:tap + 512], tap == 0, tap == 3)
                        dst = Qf[:, j, :] if j < 8 else Kf[:, j - 8, :]
                        act(dst, ps[:], AF.Silu, bias=lc(l, LC_QKB + j, 1))
                        banks.rel(bi)

                    dbg("Qf", Qf[:]); dbg("Kf", Kf[:])
                    S.mark(f"{ti}.{l}.vproj")
                    mset(V, Vt[:, :, :, 256:257], 1.0)
                    for half in range(2):
                        w = next_w("in", l, C_V + half * 512)
                        for b in range(NB):
                            bi, ps = pbank()
                            for kc in range(8):
                                mm(ps[:], Hh[:, kc, b * 128:(b + 1) * 128], w[:, kc, :], kc == 0, kc == 7)
                            src = ps[:].rearrange("p (h e) -> p h e", h=2)
                            dstv = Vt[:, b, half * 2:half * 2 + 2, 0:256]
                            if b % 2 == 0:
                                cpy(V, dstv, src)
                            else:
                                act(dstv, src, AF.Copy)
                            banks.rel(bi)
                    S.mark(f"{ti}.{l}.gates")
                    bi, ps = pbank()
                    for b in range(NB):
                        for kc in range(8):
                            mm(ps[:, b * 8:(b + 1) * 8], Hh[:, kc, b * 128:(b + 1) * 128], wif[:, l, kc, :], kc == 0, kc == 7)
                    tt(V, gif, ps[:, 0:32], lc(l, LC_BIF, 32), ALU.add)
                    banks.rel(bi)
                    gif3 = gif.rearrange("p (b g) -> p b g", b=4)
                    act(egf.rearrange("p (b g) -> p b g", b=4), gif3[:, :, 4:8], AF.Exp, scale=-1.0)
                    act(nlf, egf, AF.Ln, bias=1.0)
                    bi, ps = pbank()
                    for b in range(NB):
                        mm(ps[:, b * 8:b * 8 + 4], tri, nlf[:, b * 4:(b + 1) * 4], True, True)
                        mm(ps[:, b * 8 + 4:b * 8 + 8], ones32[:], nlf[:, b * 4:(b + 1) * 4], True, True)
                    ps3 = ps[:, 0:32].rearrange("p (b g) -> p b g", b=4)
                    ncum = ps3[:, :, 0:4]
                    ntot = ps3[:, :, 4:8]
                    v3 = lambda a: a.rearrange("p (b g) -> p b g", b=4)
                    act(v3(r16), ncum, AF.Exp, scale=-1.0, bias=-float(np.log(16.0)))
                    act(v3(ebt), ntot, AF.Exp, scale=-1.0)
                    tt(V, v3(a1), ncum, gif3[:, :, 0:4], ALU.add)
                    act(csc, a1, AF.Exp)
                    tt(V, v3(a2), ntot, v3(a1), ALU.subtract)
                    act(c2, a2, AF.Exp, scale=-1.0)
                    banks.rel(bi)
                    S.mark(f"{ti}.{l}.oza")
                    for b in range(NB):
                        blk = slice(b * 128, (b + 1) * 128)
                        bi, ps = pbank()
                        pst = ps[:].bitcast(BF16)
                        for j in range(8):
                            tr(pst[:, j * 128:(j + 1) * 128], Kf[:, j, blk], signal=(j == 7))
                        for h in range(4):
                            if b % 2 == 0:
                                ts(V, KPP[:, b, h * 256:(h + 1) * 256], pst[:, h * 256:(h + 1) * 256], c2[:, b * 4 + h:b * 4 + h + 1], ALU.mult)
                            else:
                                act(KPP[:, b, h * 256:(h + 1) * 256], pst[:, h * 256:(h + 1) * 256], AF.Copy, scale=c2[:, b * 4 + h:b * 4 + h + 1])
                        banks.rel(bi)

                    def gen_oza(l=l):
                        for half in range(2):
                            wo = next_w("in", l, C_O + half * 512)
                            for m in range(4):
                                feat_proj(wo, m, lambda ps, m=m: act(SG3[:, m, :], ps[:], AF.Sigmoid))
                                yield
                            wz = next_w("in", l, C_ZA + half * 512)
                            for m in range(4):
                                j = half * 4 + m
                                sz = tf()
                                feat_proj(wz, m, lambda ps, sz=sz: act(sz[:], ps[:], AF.Silu))
                                tt(V, GA[:, j, :], SG3[:, m, :], sz[:], ALU.mult)
                                yield
                    def gen_pb(l=l):
                        for half in range(2):
                            wpb = next_w("pb", l, half * 512)
                            for m in range(4):
                                j = half * 4 + m
                                bi, ps = pbank()
                                for kc in range(8):
                                    mm(ps[:], wpb[:, kc, m * 128:(m + 1) * 128], UB[:, kc, :], kc == 0, kc == 7)
                                act(YB[:, j, :], ps[:], AF.Copy)
                                banks.rel(bi)
                                yield

                    def gen_fill():
                        yield from gen_oza()
                        yield from gen_pb()
                    fill = gen_fill()

                    def filler(n):
                        for _ in range(n):
                            next(fill, None)

                    S.mark(f"{ti}.{l}.mlstm")
                    def stage1a(b, l=l):
                        blk = slice(b * 128, (b + 1) * 128)
                        bs, pS = pbank()
                        for h in range(4):
                            for dc in range(2):
                                mm(pS[:, h * 128:(h + 1) * 128], Kf[:, 2 * h + dc, blk], Qf[:, 2 * h + dc, blk], dc == 0, dc == 1)
                        s0 = S0T[b % 2]
                        for h in range(4):
                            gi = b * 4 + h
                            stt(V, s0[:, h, :], pS[:, h * 128:(h + 1) * 128], csc[:, gi:gi + 1], tri, ALU.mult, ALU.mult)
                        banks.rel(bs)

                    def stage1b(b, l=l):
                        blk = slice(b * 128, (b + 1) * 128)
                        s0 = S0T[b % 2]
                        pNs = []
                        for h in range(4):
                            bn, pN = pbank()
                            pNs.append((bn, pN))
                            mm(pN[:, 0:257], s0[:, h, :], Vt[:, b, h, :], True, False)
                            for dc in range(2):
                                mm(pN[:, 0:257], Qf[:, 2 * h + dc, blk], Cbf[l][:, dc, h, :], False, dc == 1)
                        for h in range(4):
                            gi = b * 4 + h
                            for dc in range(2):
                                bc, pC = pbank()
                                mm(pC[:, 0:257], KPP[:, b, h * 256 + dc * 128:h * 256 + (dc + 1) * 128], Vt[:, b, h, :], True, True)
                                stt(V, Cst[l][:, dc, h, :], Cst[l][:, dc, h, :], ebt[:, gi:gi + 1], pC[:, 0:257], ALU.mult, ALU.add)
                                banks.rel(bc)
                                act(Cbf[l][:, dc, h, :], Cst[l][:, dc, h, :], AF.Copy)
                        return pNs

                    def stage2(b, pNs):
                        for h in range(4):
                            gi = b * 4 + h
                            act(dn[:, h:h + 1], pNs[h][1][:, 256:257], AF.Abs, scale=r16[:, gi:gi + 1])
                        ts(V, dn, dn, 1.0, ALU.max)
                        S.op(V, lambda e: e.reciprocal(out=rec, in_=dn), [dn], [rec])
                        tt(V, hs, r16[:, b * 4:(b + 1) * 4], rec, ALU.mult)
                        h1 = H1[b % 2]
                        for h in range(4):
                            bn, pN = pNs[h]
                            act(junk[:, 0:256], pN[:, 0:256], AF.Square, scale=hs[:, h:h + 1], accum_out=ssq2[:, h:h + 1])
                            act(h1[:, h, :], pN[:, 0:256], AF.Copy, scale=hs[:, h:h + 1])
                            banks.rel(bn)

                    def stage3a(b):
                        act(lnv2, ssq2, AF.Ln, bias=EPS, scale=1.0 / 256.0)
                        act(rs2, lnv2, AF.Exp, scale=-0.5)
                        h1 = H1[b % 2]
                        hm = HM[b % 2]
                        for h in range(4):
                            ts(V, hm[:, h, :], h1[:, h, :], rs2[:, h:h + 1], ALU.mult)

                    def stage3b(b, l=l):
                        blk = slice(b * 128, (b + 1) * 128)
                        hm = HM[b % 2]
                        bt, pT = pbank()
                        ptt = pT[:].bitcast(BF16)
                        for h in range(4):
                            for ec in range(2):
                                c = 2 * h + ec
                                tr(ptt[:, c * 128:(c + 1) * 128], hm[:, h, ec * 128:(ec + 1) * 128], signal=(c == 7))
                        for c in range(8):
                            if b % 2 == 0:
                                ts(V, OA[:, c, blk], ptt[:, c * 128:(c + 1) * 128], lc(l, LC_MG + c, 1), ALU.mult)
                            else:
                                act(OA[:, c, blk], ptt[:, c * 128:(c + 1) * 128], AF.Copy, scale=lc(l, LC_MG + c, 1))
                        banks.rel(bt)

                    stage1a(0)
                    for b in range(NB):
                        pNs = stage1b(b)
                        if b > 0:
                            stage3a(b - 1)
                        filler(3)
                        if b > 0:
                            stage3b(b - 1)
                        if b + 1 < NB:
                            stage1a(b + 1)
                        stage2(b, pNs)
                        filler(2 if b < NB - 1 else 1)
                    stage3a(NB - 1)
                    filler(2)
                    stage3b(NB - 1)
                    for _ in fill:
                        pass
                    dbg("Vt", Vt[:]); dbg("small", small[:]); dbg("GA", GA[:])
                    for c in range(8):
                        tt(V, OA[:, c, :], OA[:, c, :], GA[:, c, :], ALU.mult)

                    dbg("KPP", KPP[:]); dbg("OA", OA[:]); dbg("Cst", Cst[l][:])
                    S.mark(f"{ti}.{l}.merge")
                    for half in range(2):
                        wga = next_w("in", l, C_GA + half * 512)
                        for m in range(4):
                            feat_proj(wga, m, lambda ps, m=m: act(SGA[:, m, :], ps[:], AF.Sigmoid))
                        wpa = next_w("pa", l, half * 512)
                        for m in range(4):
                            bi, ps = pbank()
                            for kc in range(8):
                                mm(ps[:], wpa[:, kc, m * 128:(m + 1) * 128], OA[:, kc, :], kc == 0, kc == 7)
                            tt(V, SGA[:, m, :], ps[:], SGA[:, m, :], ALU.mult)
                            banks.rel(bi)
                        wgb = next_w("in", l, C_GB + half * 512)
                        for m in range(4):
                            feat_proj(wgb, m, lambda ps, m=m: act(SGB[:, m, :], ps[:], AF.Sigmoid))
                        for m in range(4):
                            j = half * 4 + m
                            tt(V, SGB[:, m, :], YB[:, j, :], SGB[:, m, :], ALU.mult)
                            tt(P, MG[:, j, :], SGA[:, m, :], SGB[:, m, :], ALU.add)
                    dbg("MG", MG[:])
                    S.mark(f"{ti}.{l}.outproj")
                    for half in range(2):
                        w = next_w("out", l, half * 512)
                        for b in range(NB):
                            bi, ps = pbank()
                            for kc in range(8):
                                mm(ps[:], MG[:, kc, b * 128:(b + 1) * 128], w[:, kc, :], kc == 0, kc == 7)
                            xs = x[:, b, half * 512:(half + 1) * 512]
                            tt(V, xs, ps[:], xs, ALU.add)
                            banks.rel(bi)

                    dbg("x1", x[:])
                S.mark(f"{ti}.fin")
                rms_stats(x)
                for b in range(NB):
                    stt(V, OUT[:, b, :], x[:, b, :], rstd[:, b:b + 1], fg, ALU.mult, ALU.mult)
                r0 = (s * n_tiles + t) * TT
                S.dma(P, f"o{ti % 2}", y_d[r0:r0 + TT, :].rearrange("(b p) d -> p b d", p=128), OUT[:], track_out=False)


        S.dry = True
        main_loop()
        S.dry = False
        banks.reset(); tmpf_i[0] = 0; tmpb_i[0] = 0; dbg_done.clear()
        wstate["loaded"] = 0; wstate["use"] = 0; wstate.pop("cast1_waited", None)
        S.wait_all("sync", ["cast0"] + [f"w{i}" for i in range(NW)])
        emit_xload(0)
        main_loop()

        S.wait_all(P, ["o0", "o1"])
        assert wstate["use"] == len(wlist)

        with nc.Block() as block:
            @block.tensor
            def _(e):
                for f in S.q["tensor"]:
                    f(e)

            @block.scalar
            def _(e):
                for f in S.q["scalar"]:
                    f(e)

            @block.vector
            def _(e):
                for f in S.q["vector"]:
                    f(e)

            @block.gpsimd
            def _(e):
                for f in S.q["gpsimd"]:
                    f(e)

            @block.sync
            def _(e):
                for f in S.q["sync"]:
                    f(e)
        build_program.last_ninst = S.ninst
        build_program.marks = S.marks
    return nc


def make_cpack(norm_g, b_if, conv_qk_w, conv_qk_b, mhn_g, dw_w, dw_b, ln_g, ln_b, w_in, final_g):
    cpk = np.zeros((128, NCONST), np.float32)

    def pc(v):
        return np.ascontiguousarray(v.reshape(-1, 128).T)

    for l in range(L):
        o = l * LC_SIZE
        cpk[:, o + LC_NG:o + LC_NG + 8] = pc(norm_g[l])
        cpk[:, o + LC_QKW:o + LC_QKW + 64] = conv_qk_w[l].reshape(4, 16, 128).transpose(2, 1, 0).reshape(128, 64)
        cpk[:, o + LC_QKB:o + LC_QKB + 16] = pc(conv_qk_b[l])
        cpk[:, o + LC_MG:o + LC_MG + 8] = pc(mhn_g[l])
        wpad = np.concatenate([dw_w[l], np.zeros((1, 1024), np.float32)], axis=0)
        wr = wpad.reshape(8, 4, 8, 4, 32)
        cpk[:, o + LC_DWW:o + LC_DWW + 256] = wr.transpose(1, 4, 2, 0, 3).reshape(128, 256)
        cpk[:, o + LC_DWB:o + LC_DWB + 8] = pc(dw_b[l])
        cpk[:, o + LC_LNG:o + LC_LNG + 8] = pc(ln_g[l])
        cpk[:, o + LC_LNB:o + LC_LNB + 8] = pc(ln_b[l])
        cpk[:, o + LC_BIF:o + LC_BIF + 32] = np.tile(b_if[l][None, :], (128, 4))
        cpk[:, o + LC_WIF:o + LC_WIF + 64] = w_in[l][:, C_IF:C_IF + 8].reshape(8, 128, 8).transpose(1, 0, 2).reshape(128, 64)
    cpk[:, GC_FG:GC_FG + 1024] = np.tile(final_g[None, :], (128, 1))
    cpk[:, GC_TRI:GC_TRI + 128] = np.triu(np.ones((128, 128), np.float32))
    cpk[:, GC_ID:GC_ID + 128] = np.eye(128, dtype=np.float32)
    cpk[:, GC_IDR:GC_IDR + 32] = np.tile(np.eye(32, dtype=np.float32), (4, 1))
    return cpk


def kernel(x, norm_g, w_in, b_if, conv_qk_w, conv_qk_b, mhn_g, dw_w, dw_b, ln_g, ln_b,
           w_pa, w_pb, w_out, final_g, _n_cores=8):
    x = np.asarray(x, np.float32)
    B, Sq, Dm = x.shape
    n_cores = _n_cores
    spc = B // n_cores
    n_tiles = Sq // TT
    f = lambda a: np.ascontiguousarray(np.asarray(a, np.float32))
    w_in, w_pa, w_pb, w_out = f(w_in), f(w_pa), f(w_pb), f(w_out)
    cpk = make_cpack(f(norm_g), f(b_if), f(conv_qk_w), f(conv_qk_b), f(mhn_g), f(dw_w), f(dw_b),
                     f(ln_g), f(ln_b), w_in, f(final_g))
    nc = build_program(spc, n_tiles)
    in_maps = []
    for c in range(n_cores):
        xs = np.ascontiguousarray(x[c * spc:(c + 1) * spc].reshape(spc * Sq, Dm))
        in_maps.append({"x": xs, "w_in": w_in, "w_pa": w_pa, "w_pb": w_pb, "w_out": w_out, "cpack": cpk})
    res = run_bass_kernel_spmd(nc, in_maps, core_ids=list(range(n_cores)))
    out = np.concatenate([np.asarray(r["y"]).reshape(spc, Sq, Dm) for r in res.results], axis=0)
    return out.astype(np.float32)
```

```python
import numpy as np
import concourse.bass as bass
import concourse.mybir as mybir
from concourse.bass_utils import run_bass_kernel_spmd

F32 = mybir.dt.float32
BF16 = mybir.dt.bfloat16
AF = mybir.ActivationFunctionType
ALU = mybir.AluOpType

D = 1024
NPROJ = 10248
SEQ = 2048
TT = 512
NB = 4
L = 2
EPS = 1e-6
C_Q, C_K, C_V, C_O, C_IF, C_ZA, C_GLA, C_GLB, C_ZB, C_GA, C_GB = (
    0, 1024, 2048, 3072, 4096, 4104, 5128, 6152, 7176, 8200, 9224)

LC_NG, LC_QKW, LC_QKB, LC_MG, LC_DWW, LC_DWB, LC_LNG, LC_LNB, LC_BIF, LC_WIF = (
    0, 8, 72, 88, 96, 352, 360, 368, 376, 408)
LC_SIZE = 472
GC_FG = L * LC_SIZE
GC_TRI = GC_FG + 1024
GC_ID = GC_TRI + 128
GC_IDR = GC_ID + 128
NCONST = GC_IDR + 32

SLOT = 4352


class Sched:
    ENG = ("tensor", "scalar", "vector", "gpsimd", "sync")

    def __init__(self, nc):
        self.nc = nc
        self.q = {e: [] for e in self.ENG}
        self.sems = {}
        self.cnt = {}
        self.waited = {e: {} for e in self.ENG}
        self.rec = {}
        self.ninst = 0
        self.nops = {e: 0 for e in self.ENG}
        self.marks = []
        self.dry = False

    def add_sem(self, key, sem):
        self.sems[key] = sem
        self.cnt[key] = 0

    def ap_range(self, ap):
        name = ap.name
        dims = ap.ap
        pstep = dims[0][0]
        off = ap.offset % pstep if pstep > 0 else ap.offset
        ext = 1
        for st, c in dims[1:]:
            ext += abs(st) * (c - 1)
        es = DT_SIZE[ap.dtype]
        lo = off * es
        hi = (off + ext) * es
        if str(ap.space) == "PSUM" or "PSUM" in str(ap.space):
            lo = (lo // 2048) * 2048
            hi = ((hi + 2047) // 2048) * 2048
        return name, lo, hi

    def _deps(self, eng, reads, writes):
        deps = {}

        def need(tk):
            if tk is None:
                return
            k, v = tk
            assert v <= self.cnt[k], ("dependency on a not-yet-emitted signal", k, v, self.cnt[k])
            if deps.get(k, 0) < v:
                deps[k] = v

        for ap in reads:
            name, lo, hi = self.ap_range(ap)
            is_ps = "PSUM" in str(ap.space)
            for r in self.rec.get(name, []):
                if r[0] < hi and lo < r[1]:
                    need(r[2])
                    if is_ps:
                        for k, v in r[3].items():
                            if k != eng:
                                need((k, v))
        for ap in writes:
            name, lo, hi = self.ap_range(ap)
            for r in self.rec.get(name, []):
                if r[0] < hi and lo < r[1]:
                    w = r[2]
                    if w is not None and not (w[0] == eng and eng in SAME_ENG_OK):
                        need(w)
                    for k, v in r[3].items():
                        if not (k == eng and eng in SAME_ENG_OK):
                            need((k, v))
        return deps

    def _update(self, reads, writes, tk):
        for ap in reads:
            name, lo, hi = self.ap_range(ap)
            lst = self.rec.setdefault(name, [])
            new = []
            covered = []
            for r in lst:
                if r[0] < hi and lo < r[1]:
                    if r[0] < lo:
                        new.append([r[0], lo, r[2], dict(r[3])])
                    if hi < r[1]:
                        new.append([hi, r[1], r[2], dict(r[3])])
                    a, b = max(r[0], lo), min(r[1], hi)
                    rd = dict(r[3])
                    if rd.get(tk[0], 0) < tk[1]:
                        rd[tk[0]] = tk[1]
                    new.append([a, b, r[2], rd])
                    covered.append((a, b))
                else:
                    new.append(r)
            covered.sort()
            cur = lo
            for a, b in covered:
                if a > cur:
                    new.append([cur, a, None, {tk[0]: tk[1]}])
                cur = max(cur, b)
            if cur < hi:
                new.append([cur, hi, None, {tk[0]: tk[1]}])
            self.rec[name] = new
        for ap in writes:
            name, lo, hi = self.ap_range(ap)
            lst = self.rec.setdefault(name, [])
            new = []
            for r in lst:
                if r[0] < hi and lo < r[1]:
                    if r[0] < lo:
                        new.append([r[0], lo, r[2], dict(r[3])])
                    if hi < r[1]:
                        new.append([hi, r[1], r[2], dict(r[3])])
                else:
                    new.append(r)
            new.append([lo, hi, tk, {}])
            self.rec[name] = new

    def _emit_waits(self, eng, deps):
        for k, v in deps.items():
            if self.waited[eng].get(k, 0) >= v:
                continue
            self.waited[eng][k] = v
            sem = self.sems[k]
            self.q[eng].append(lambda e, sem=sem, v=v: e.wait_ge(sem, v))
            self.ninst += 1

    def op(self, eng, fn, reads=(), writes=(), signal=True):
        if self.dry:
            return None
        deps = self._deps(eng, reads, writes)
        self._emit_waits(eng, deps)
        tk = (eng, self.cnt[eng] + 1)
        if signal:
            self.cnt[eng] += 1
            sem = self.sems[eng]
            self.q[eng].append(lambda e, fn=fn, sem=sem: fn(e).then_inc(sem, 1))
        else:
            self.q[eng].append(lambda e, fn=fn: fn(e))
        self.ninst += 1
        self.nops[eng] += 1
        self._update(reads, writes, tk)
        return tk

    def mark(self, label):
        if self.dry:
            return
        self.marks.append((label, dict(self.nops)))

    def dma(self, eng, semkey, out, in_, track_out=True, track_in=True, serialize=True, **kw):
        if self.dry:
            return None
        onchip = lambda a: str(a.space) in ("SB", "PSUM")
        reads = [in_] if (track_in and onchip(in_)) else []
        writes = [out] if (track_out and onchip(out)) else []
        deps = self._deps("dma", reads, writes)
        if serialize and self.cnt[semkey] > 0:
            deps[semkey] = max(deps.get(semkey, 0), self.cnt[semkey])
        self._emit_waits(eng, deps)
        self.cnt[semkey] += 16
        tk = (semkey, self.cnt[semkey])
        sem = self.sems[semkey]
        self.q[eng].append(lambda e, out=out, in_=in_, sem=sem, kw=kw: e.dma_start(out=out, in_=in_, **kw).then_inc(sem, 16))
        self.ninst += 1
        self._update(reads, writes, tk)
        return tk

    def dma_group(self, eng, semkey, pairs):
        if self.dry:
            return None
        reads = [i for _, i in pairs]
        writes = [o for o, _ in pairs]
        deps = self._deps("dma", reads, writes)
        if self.cnt[semkey] > 0:
            deps[semkey] = max(deps.get(semkey, 0), self.cnt[semkey])
        self._emit_waits(eng, deps)
        sem = self.sems[semkey]
        for out, in_ in pairs:
            self.cnt[semkey] += 16
            self.q[eng].append(lambda e, out=out, in_=in_, sem=sem: e.dma_start(out=out, in_=in_).then_inc(sem, 16))
            self.ninst += 1
        tk = (semkey, self.cnt[semkey])
        self._update(reads, writes, tk)
        return tk

    def wait_all(self, eng, keys):
        if self.dry:
            return
        deps = {k: self.cnt[k] for k in keys if self.cnt[k] > 0}
        self._emit_waits(eng, deps)


DT_SIZE = {F32: 4, BF16: 2}
SAME_ENG_OK = ("tensor",)


class Banks:
    def __init__(self, banks):
        self.banks = banks
        self.free = list(range(len(banks)))

    def get(self):
        assert self.free, "out of PSUM banks"
        i = self.free.pop(0)
        return i

    def rel(self, i):
        assert i not in self.free
        self.free.append(i)

    def reset(self):
        self.free = list(range(len(self.banks)))


def build_program(n_seq, n_tiles, prepass_pieces=1024, debug=None):
    nc = bass.Bass("TRN2", target_bir_lowering=False)
    ntok = n_seq * n_tiles * TT
    x_d = nc.dram_tensor("x", [ntok, D], F32, kind="ExternalInput").ap()
    y_d = nc.dram_tensor("y", [ntok, D], F32, kind="ExternalOutput").ap()
    win_d = nc.dram_tensor("w_in", [L, D, NPROJ], F32, kind="ExternalInput").ap()
    wpa_d = nc.dram_tensor("w_pa", [L, D, D], F32, kind="ExternalInput").ap()
    wpb_d = nc.dram_tensor("w_pb", [L, D, D], F32, kind="ExternalInput").ap()
    wout_d = nc.dram_tensor("w_out", [L, D, D], F32, kind="ExternalInput").ap()
    cp_d = nc.dram_tensor("cpack", [128, NCONST], F32, kind="ExternalInput").ap()
    s_in = nc.dram_tensor("s_in", [L, D, NPROJ], BF16, kind="Internal").ap()
    s_pa = nc.dram_tensor("s_pa", [L, D, D], BF16, kind="Internal").ap()
    s_pb = nc.dram_tensor("s_pb", [L, D, D], BF16, kind="Internal").ap()
    s_out = nc.dram_tensor("s_out", [L, D, D], BF16, kind="Internal").ap()
    s_dg = nc.dram_tensor("s_dg", [L, 4, 128, 4096], BF16, kind="Internal").ap()

    import contextlib
    es = contextlib.ExitStack()
    with es:
        def sb(name, shape, dt):
            return es.enter_context(nc.sbuf_tensor(name, shape, dt))

        def sem(name):
            return es.enter_context(nc.semaphore(name))

        xb = [sb(f"xb{i}", [128, NB, D], F32) for i in range(2)]
        arena = sb("arena", [128, 9 * SLOT], BF16)
        NW = 4
        wbuf = [sb(f"wbuf{i}", [128, 8, 512], BF16) for i in range(NW)]
        Cst = [sb(f"Cst{l}", [128, 2, 4, 257], F32) for l in range(L)]
        Cbf = [sb(f"Cbf{l}", [128, 2, 4, 257], BF16) for l in range(L)]
        cp = sb("cp", [128, NCONST], F32)
        ident = sb("ident", [128, 128], BF16)
        identr = sb("identr", [128, 32], BF16)
        onesb = sb("onesb", [128, 128], BF16)
        ones32 = sb("ones32", [128, 128], F32)
        wif = sb("wif", [128, L, 8, 8], BF16)
        hist_u = [sb(f"hist_u{l}", [128, 8, 30], BF16) for l in range(L)]
        hist_qk = [sb(f"hist_qk{l}", [128, 16, 3], BF16) for l in range(L)]
        NTMP = 4
        tmpf = [sb(f"tmpf{i}", [128, 512], F32) for i in range(NTMP)]
        NTB = 4
        tmpb = [sb(f"tmpb{i}", [128, 512], BF16) for i in range(NTB)]
        junk = sb("junk", [128, 1024], BF16)
        small = sb("small", [128, 256], F32)
        S0T = [sb(f"S0T{i}", [128, 4, 128], BF16) for i in range(2)]
        H1 = [sb(f"H1_{i}", [128, 4, 256], F32) for i in range(2)]
        HM = [sb(f"HM{i}", [128, 4, 256], BF16) for i in range(2)]

        psb = [es.enter_context(nc.psum_tensor(f"ps{i}", [128, 512], F32)) for i in range(8)]
        banks = Banks(psb)

        S = Sched(nc)
        for e in ("tensor", "scalar", "vector", "gpsimd"):
            S.add_sem(e, sem("s_" + e))
        for i in range(NW):
            S.add_sem(f"w{i}", sem(f"s_w{i}"))
        for i in range(2):
            S.add_sem(f"x{i}", sem(f"s_x{i}"))
            S.add_sem(f"o{i}", sem(f"s_o{i}"))
        S.add_sem("cast0", sem("s_cast0"))
        S.add_sem("cast1", sem("s_cast1"))
        NBB = 8
        for i in range(NBB):
            S.add_sem(f"bb{i}", sem(f"s_bb{i}"))
        S.add_sem("dgst", sem("s_dgst"))
        S.add_sem("dbg", sem("s_dbg"))
        dbg_done = set()

        def dbg(name, ap):
            if debug is None or name in dbg_done or S.dry:
                return
            dbg_done.add(name)
            shp = list(ap.shape)
            dt_ = nc.dram_tensor("dbg_" + name, shp, ap.dtype, kind="ExternalOutput").ap()
            debug[name] = shp
            S.dma(P, "dbg", dt_, ap, track_out=False)
        S.add_sem("const", sem("s_const"))

        def av(slot0, nelem_bf16, dt=BF16, off=0):
            a = arena[:, slot0 * SLOT + off: slot0 * SLOT + off + nelem_bf16]
            if dt == F32:
                a = a.bitcast(F32)
            return a

        Hh = av(0, 4096).rearrange("p (c t) -> p c t", c=8)
        XN = av(1, 4096).rearrange("p (b d) -> p b d", b=4)
        Ue = av(1, 8 * 543).rearrange("p (c t) -> p c t", c=8)
        BPK = av(5, 8 * 4 * 540).rearrange("p (j g m) -> p j g m", j=8, g=4)
        UC = av(2, 8192, F32).rearrange("p (c t) -> p c t", c=8)
        ZB = av(4, 4096).rearrange("p (c t) -> p c t", c=8)
        UB = av(5, 4096).rearrange("p (c t) -> p c t", c=8)
        LNS = av(6, 4096, F32).rearrange("p (m t) -> p m t", m=4)
        QKe = av(7, 16 * 515).rearrange("p (c t) -> p c t", c=16)
        Qf = av(1, 4096).rearrange("p (c t) -> p c t", c=8)
        Kf = av(2, 4096).rearrange("p (c t) -> p c t", c=8)
        GA = av(3, 4096).rearrange("p (c t) -> p c t", c=8)
        Vt = av(4, 4 * 4 * 257).rearrange("p (b h e) -> p b h e", b=4, h=4)
        KPP = av(6, 4096).rearrange("p (b d) -> p b d", b=4)
        OA = av(7, 4096).rearrange("p (c t) -> p c t", c=8)
        MG = av(3, 4096).rearrange("p (c t) -> p c t", c=8)
        OUT = av(7, 8192, F32).rearrange("p (b d) -> p b d", b=4)
        SG2 = av(2, 4096, F32).rearrange("p (m t) -> p m t", m=4)
        YB = av(8, 4096).rearrange("p (c t) -> p c t", c=8)
        SG3 = av(8, 4096, F32).rearrange("p (m t) -> p m t", m=4)
        SGA = av(1, 4096, F32).rearrange("p (m t) -> p m t", m=4)
        SGB = av(2, 4096, F32).rearrange("p (m t) -> p m t", m=4)

        def sm(lo, n):
            return small[:, lo:lo + n]
        ssq = sm(0, 4)
        lnv = sm(4, 4)
        rstd = sm(8, 4)
        gif = sm(16, 32)
        egf = sm(48, 16)
        nlf = sm(64, 16)
        r16 = sm(80, 16)
        ebt = sm(96, 16)
        a1 = sm(112, 16)
        csc = sm(128, 16)
        a2 = sm(144, 16)
        c2 = sm(160, 16)
        dn = sm(176, 4)
        rec = sm(180, 4)
        hs = sm(184, 4)
        ssq2 = sm(188, 4)
        lnv2 = sm(192, 4)
        rs2 = sm(196, 4)
        hs2 = sm(200, 4)

        def lc(l, off, n):
            return cp[:, l * LC_SIZE + off: l * LC_SIZE + off + n]

        V, A, P, T = "vector", "scalar", "gpsimd", "tensor"

        def act(out, in_, func, bias=None, scale=None, accum_out=None, extra_reads=()):
            kw = {}
            reads = [in_] + list(extra_reads)
            if bias is not None:
                kw["bias"] = bias
                if not isinstance(bias, (int, float)):
                    reads.append(bias)
            if scale is not None:
                kw["scale"] = scale
                if not isinstance(scale, (int, float)):
                    reads.append(scale)
            writes = [out]
            if accum_out is not None:
                kw["accum_out"] = accum_out
                writes.append(accum_out)
            return S.op(A, lambda e: e.activation(out=out, in_=in_, func=func, **kw), reads, writes)

        def tt(eng, out, in0, in1, op):
            return S.op(eng, lambda e: e.tensor_tensor(out=out, in0=in0, in1=in1, op=op), [in0, in1], [out])

        def ts(eng, out, in0, s1, op0, s2=None, op1=None):
            reads = [in0]
            if not isinstance(s1, (int, float)):
                reads.append(s1)
            if s2 is not None and not isinstance(s2, (int, float)):
                reads.append(s2)
            if op1 is None:
                return S.op(eng, lambda e: e.tensor_scalar(out=out, in0=in0, scalar1=s1, scalar2=None, op0=op0), reads, [out])
            return S.op(eng, lambda e: e.tensor_scalar(out=out, in0=in0, scalar1=s1, scalar2=s2, op0=op0, op1=op1), reads, [out])

        def stt(eng, out, in0, scalar, in1, op0, op1):
            reads = [in0, in1]
            if not isinstance(scalar, (int, float)):
                reads.append(scalar)
            return S.op(eng, lambda e: e.scalar_tensor_tensor(out=out, in0=in0, scalar=scalar, in1=in1, op0=op0, op1=op1), reads, [out])

        def cpy(eng, out, in_):
            return S.op(eng, lambda e: e.tensor_copy(out=out, in_=in_), [in_], [out])

        def mset(eng, out, val):
            return S.op(eng, lambda e: e.memset(out, val), [], [out])

        def mm(out, lhsT, rhs, start, stop, signal=None, tile_position=None):
            if signal is None:
                signal = stop
            if tile_position is not None:
                return S.op(T, lambda e: e.matmul(out, lhsT=lhsT, rhs=rhs, start=start, stop=stop, tile_position=tile_position),
                            [lhsT, rhs], [out], signal=signal)
            return S.op(T, lambda e: e.matmul(out, lhsT=lhsT, rhs=rhs, start=start, stop=stop), [lhsT, rhs], [out], signal=signal)

        def tr(out, in_, signal):
            return S.op(T, lambda e: e.transpose(out, in_, ident[:]), [in_, ident[:]], [out], signal=signal)

        def pbank():
            i = banks.get()
            return i, psb[i]

        tmpf_i = [0]
        def tf():
            tmpf_i[0] = (tmpf_i[0] + 1) % NTMP
            return tmpf[tmpf_i[0]]
        tmpb_i = [0]
        def tb():
            tmpb_i[0] = (tmpb_i[0] + 1) % NTB
            return tmpb[tmpb_i[0]]
        for l in range(L):
            c0 = 0
            while c0 < NPROJ:
                c1 = min(NPROJ, c0 + prepass_pieces)
                S.dma(P, f"cast{l}", s_in[l, :, c0:c1], win_d[l, :, c0:c1], track_out=False, track_in=False, serialize=False)
                c0 = c1
            for sd, wd in ((s_pa, wpa_d), (s_pb, wpb_d), (s_out, wout_d)):
                S.dma(P, f"cast{l}", sd[l], wd[l], track_out=False, track_in=False, serialize=False)
        S.dma("sync", "const", cp[:], cp_d, serialize=False)
        cpy(V, ident[:], cp[:, GC_ID:GC_ID + 128])
        cpy(V, identr[:], cp[:, GC_IDR:GC_IDR + 32])
        mset(V, onesb[:], 1.0 / 1024.0)
        mset(V, ones32[:], 1.0)
        mset(V, small[:], 0.0)
        for l in range(L):
            cpy(V, wif[:, l, :, :], lc(l, LC_WIF, 64).rearrange("p (k n) -> p k n", k=8))
        tri = cp[:, GC_TRI:GC_TRI + 128]
        fg = cp[:, GC_FG:GC_FG + 1024]
        def dg_ncols(idx):
            return 32 * 128
        k = 0
        for l in range(L):
            for idx in range(4):
                stag = wbuf[k % NW][:].rearrange("p k n -> p (k n)")
                stsem = f"w{k % NW}"
                k += 1
                if idx < 2:
                    for kk in range(128):
                        jj, pg = kk // 32, kk % 32
                        ts(V, stag[:, kk * 32:(kk + 1) * 32], identr[:], lc(l, LC_DWW + (idx * 4 + jj) * 32 + pg, 1), ALU.mult)
                else:
                    for jj in range(8):
                        j = (idx - 2) * 8 + jj
                        for tap in range(4):
                            sl = jj * 4 + tap
                            ts(V, stag[:, sl * 128:(sl + 1) * 128], ident[:], lc(l, LC_QKW + j * 4 + tap, 1), ALU.mult)
                n = dg_ncols(idx)
                S.dma("sync", stsem, s_dg[l, idx, :, 0:n], stag[:, 0:n])

        wlist = []
        wstate = {"loaded": 0, "use": 0}
        PF = NW - 1

        def emit_wload(i):
            kind, l, c0 = wlist[i]
            slot = i % NW
            if l == 1 and not wstate.get("cast1_waited"):
                wstate["cast1_waited"] = True
                S.wait_all("sync", ["cast1"])
            if kind == "dg":
                n = dg_ncols(c0)
                S.dma("sync", f"w{slot}", wbuf[slot][:].rearrange("p k n -> p (k n)")[:, 0:n], s_dg[l, c0, :, 0:n], track_in=False)
                return
            src = {"in": s_in, "pa": s_pa, "pb": s_pb, "out": s_out}[kind]
            srcap = src[l, :, c0:c0 + 512].rearrange("(k p) n -> p k n", p=128)
            S.dma("sync", f"w{slot}", wbuf[slot][:], srcap, track_in=False)

        def next_w(kind_, l_, c0_):
            if S.dry:
                wlist.append((kind_, l_, c0_))
                return wbuf[0]
            i = wstate["use"]
            kind, l, c0 = wlist[i]
            assert (kind, l, c0) == (kind_, l_, c0_), (kind, l, c0, kind_, l_, c0_)
            while wstate["loaded"] <= min(i + PF, len(wlist) - 1):
                emit_wload(wstate["loaded"])
                wstate["loaded"] += 1
            wstate["use"] += 1
            return wbuf[i % NW]


        tiles = [(s, t) for s in range(n_seq) for t in range(n_tiles)]

        def emit_xload(ti):
            s, t = tiles[ti]
            r0 = (s * n_tiles + t) * TT
            S.dma("sync", f"x{ti % 2}", xb[ti % 2][:], x_d[r0:r0 + TT, :].rearrange("(b p) d -> p b d", p=128))


        def feat_proj(w, m, evac):
            bi, ps = pbank()
            for kc in range(8):
                mm(ps[:], w[:, kc, m * 128:(m + 1) * 128], Hh[:, kc, :], kc == 0, kc == 7)
            evac(ps)
            banks.rel(bi)

        def rms_stats(x):
            for b in range(NB):
                act(junk[:], x[:, b, :], AF.Square, accum_out=ssq[:, b:b + 1])
            act(lnv, ssq, AF.Ln, bias=EPS, scale=1.0 / D)
            act(rstd, lnv, AF.Exp, scale=-0.5)

        def main_loop():
            for ti, (s, t) in enumerate(tiles):
                x = xb[ti % 2]
                if t == 0:
                    for l in range(L):
                        mset(P, hist_u[l][:], 0.0)
                        mset(P, hist_qk[l][:], 0.0)
                        mset(P, Cst[l][:], 0.0)
                        mset(P, Cbf[l][:], 0.0)
                for l in range(L):
                    S.mark(f"{ti}.{l}.p0")
                    rms_stats(x)
                    for b in range(NB):
                        if b % 2 == 0:
                            ts(V, XN[:, b, :], x[:, b, :], rstd[:, b:b + 1], ALU.mult)
                        else:
                            act(XN[:, b, :], x[:, b, :], AF.Copy, scale=rstd[:, b:b + 1])
                    for j in range(8):
                        bi, ps = pbank()
                        pst = ps[:].bitcast(BF16)
                        for b in range(NB):
                            tr(pst[:, b * 128:(b + 1) * 128], XN[:, b, j * 128:(j + 1) * 128], signal=(b == NB - 1))
                        act(Hh[:, j, :], pst[:, 0:512], AF.Copy, scale=lc(l, LC_NG + j, 1))
                        banks.rel(bi)
                    dbg("hT", Hh[:])
                    if l == 1 and ti + 1 < len(tiles):
                        emit_xload(ti + 1)

                    S.mark(f"{ti}.{l}.glu")
                    cpy(P, Ue[:, :, 0:30], hist_u[l][:])
                    mset(V, Ue[:, :, 542:543], 0.0)
                    for half in range(2):
                        wb = next_w("in", l, C_GLB + half * 512)
                        for m in range(4):
                            feat_proj(wb, m, lambda ps, m=m: act(SG2[:, m, :], ps[:], AF.Sigmoid))
                        wa = next_w("in", l, C_GLA + half * 512)
                        for m in range(4):
                            j = half * 4 + m
                            feat_proj(wa, m, lambda ps, m=m, j=j: tt(V, Ue[:, j, 30:542], ps[:], SG2[:, m, :], ALU.mult))
                        S.dma_group(A, f"bb{half % NBB}",
                                    [(BPK[32 * i:32 * i + 32, 4 * half:4 * half + 4, g, :],
                                      Ue[32 * g:32 * g + 32, 4 * half:4 * half + 4, i:i + 540])
                                     for g in range(4) for i in range(4)])
                    cpy(P, hist_u[l][:], Ue[:, :, 512:542])
                    dbg("Ue", Ue[:])
                    S.mark(f"{ti}.{l}.zb")
                    for half in range(2):
                        wz = next_w("in", l, C_ZB + half * 512)
                        for m in range(4):
                            j = half * 4 + m
                            feat_proj(wz, m, lambda ps, j=j: act(ZB[:, j, :], ps[:], AF.Silu))
                    S.mark(f"{ti}.{l}.dwconv")
                    bm, psm = pbank()
                    bq, psq = pbank()
                    pend = None
                    for j in range(8):
                        bi, ps = pbank()
                        if j % 4 == 0:
                            dgt = next_w("dg", l, j // 4)[:].rearrange("p k n -> p (k n)")
                        for p in range(8):
                            for g in range(4):
                                kk = ((j % 4) * 8 + p) * 4 + g
                                mm(ps[32 * g:32 * g + 32, :], dgt[:, kk * 32:(kk + 1) * 32], BPK[:, j, g, 4 * p:4 * p + 512],
                                   p == 0, p == 7, signal=(p == 7 and g == 3), tile_position=(0, 32 * g))
                        act(UC[:, j, :], ps[:], AF.Identity, bias=lc(l, LC_DWB + j, 1))
                        ucb = tb()
                        ucs = tb()
                        act(ucb[:], ps[:], AF.Identity, bias=lc(l, LC_DWB + j, 1))
                        act(ucs[:], ps[:], AF.Square, bias=lc(l, LC_DWB + j, 1))
                        banks.rel(bi)
                        if pend is not None:
                            pb_, ps_, pj = pend
                            mm(psm[:], onesb[:], pb_[:], pj == 0, False)
                            mm(psq[:], onesb[:], ps_[:], pj == 0, False)
                        pend = (ucb, ucs, j)
                    pb_, ps_, pj = pend
                    mm(psm[:], onesb[:], pb_[:], False, True)
                    mm(psq[:], onesb[:], ps_[:], False, True)
                    S.mark(f"{ti}.{l}.ln")
                    mean_sb = LNS[:, 0, :]; msq = LNS[:, 1, :]; var = LNS[:, 1, :]; rstdL = LNS[:, 2, :]; nmr = LNS[:, 3, :]
                    act(mean_sb, psm[:], AF.Identity)
                    act(msq, psm[:], AF.Square)
                    tt(V, var, psq[:], msq, ALU.subtract)
                    banks.rel(bm); banks.rel(bq)
                    act(var, var, AF.Ln, bias=EPS)
                    act(rstdL, var, AF.Exp, scale=-0.5)
                    stt(V, nmr, mean_sb, -1.0, rstdL, ALU.mult, ALU.mult)
                    def ln_chunk(j, l=l, rstdL=rstdL, nmr=nmr):
                        t1 = tf()
                        tt(V, t1[:], UC[:, j, :], rstdL, ALU.mult)
                        tt(P, t1[:], t1[:], nmr, ALU.add)
                        act(t1[:], t1[:], AF.Silu, bias=lc(l, LC_LNB + j, 1), scale=lc(l, LC_LNG + j, 1))
                        tt(V, UB[:, j, :], t1[:], ZB[:, j, :], ALU.mult)

                    S.mark(f"{ti}.{l}.qkproj")
                    cpy(P, QKe[:, :, 0:3], hist_qk[l][:])
                    for qk in range(2):
                        for half in range(2):
                            w = next_w("in", l, (C_Q if qk == 0 else C_K) + half * 512)
                            for m in range(4):
                                j = qk * 8 + half * 4 + m
                                feat_proj(w, m, lambda ps, j=j: cpy(V if j % 2 == 0 else A, QKe[:, j, 3:515], ps[:]) if j % 2 == 0
                                          else act(QKe[:, j, 3:515], ps[:], AF.Copy))
                                if j % 2 == 1:
                                    ln_chunk(j // 2)
                    dbg("UC", UC[:]); dbg("ZB", ZB[:]); dbg("UB", UB[:])
                    cpy(P, hist_qk[l][:], QKe[:, :, 512:515])
                    S.mark(f"{ti}.{l}.qkconv")
                    for j in range(16):
                        if j % 8 == 0:
                            dgt = next_w("dg", l, 2 + j // 8)[:].rearrange("p k n -> p (k n)")
                        bi, ps = pbank()
                        for tap in range(4):
                            sl = (j % 8) * 4 + tap
                            mm(ps[:], dgt[:, sl * 128:(sl + 1) * 128], QKe[:, j, tap:tap + 512], tap == 0, tap == 3)
                        dst = Qf[:, j, :] if j < 8 else Kf[:, j - 8, :]
                        act(dst, ps[:], AF.Silu, bias=lc(l, LC_QKB + j, 1))
                        banks.rel(bi)

                    dbg("Qf", Qf[:]); dbg("Kf", Kf[:])
                    S.mark(f"{ti}.{l}.vproj")
                    mset(V, Vt[:, :, :, 256:257], 1.0)
                    for half in range(2):
                        w = next_w("in", l, C_V + half * 512)
                        for b in range(NB):
                            bi, ps = pbank()
                            for kc in range(8):
                                mm(ps[:], Hh[:, kc, b * 128:(b + 1) * 128], w[:, kc, :], kc == 0, kc == 7)
                            src = ps[:].rearrange("p (h e) -> p h e", h=2)
                            dstv = Vt[:, b, half * 2:half * 2 + 2, 0:256]
                            if b % 2 == 0:
                                cpy(V, dstv, src)
                            else:
                                act(dstv, src, AF.Copy)
                            banks.rel(bi)
                    S.mark(f"{ti}.{l}.gates")
                    bi, ps = pbank()
                    for b in range(NB):
                        for kc in range(8):
                            mm(ps[:, b * 8:(b + 1) * 8], Hh[:, kc, b * 128:(b + 1) * 128], wif[:, l, kc, :], kc == 0, kc == 7)
                    tt(V, gif, ps[:, 0:32], lc(l, LC_BIF, 32), ALU.add)
                    banks.rel(bi)
                    gif3 = gif.rearrange("p (b g) -> p b g", b=4)
                    act(egf.rearrange("p (b g) -> p b g", b=4), gif3[:, :, 4:8], AF.Exp, scale=-1.0)
                    act(nlf, egf, AF.Ln, bias=1.0)
                    bi, ps = pbank()
                    for b in range(NB):
                        mm(ps[:, b * 8:b * 8 + 4], tri, nlf[:, b * 4:(b + 1) * 4], True, True)
                        mm(ps[:, b * 8 + 4:b * 8 + 8], ones32[:], nlf[:, b * 4:(b + 1) * 4], True, True)
                    ps3 = ps[:, 0:32].rearrange("p (b g) -> p b g", b=4)
                    ncum = ps3[:, :, 0:4]
                    ntot = ps3[:, :, 4:8]
                    v3 = lambda a: a.rearrange("p (b g) -> p b g", b=4)
                    act(v3(r16), ncum, AF.Exp, scale=-1.0, bias=-float(np.log(16.0)))
                    act(v3(ebt), ntot, AF.Exp, scale=-1.0)
                    tt(V, v3(a1), ncum, gif3[:, :, 0:4], ALU.add)
                    act(csc, a1, AF.Exp)
                    tt(V, v3(a2), ntot, v3(a1), ALU.subtract)
                    act(c2, a2, AF.Exp, scale=-1.0)
                    banks.rel(bi)
                    S.mark(f"{ti}.{l}.oza")
                    for b in range(NB):
                        blk = slice(b * 128, (b + 1) * 128)
                        bi, ps = pbank()
                        pst = ps[:].bitcast(BF16)
                        for j in range(8):
                            tr(pst[:, j * 128:(j + 1) * 128], Kf[:, j, blk], signal=(j == 7))
                        for h in range(4):
                            if b % 2 == 0:
                                ts(V, KPP[:, b, h * 256:(h + 1) * 256], pst[:, h * 256:(h + 1) * 256], c2[:, b * 4 + h:b * 4 + h + 1], ALU.mult)
                            else:
                                act(KPP[:, b, h * 256:(h + 1) * 256], pst[:, h * 256:(h + 1) * 256], AF.Copy, scale=c2[:, b * 4 + h:b * 4 + h + 1])
                        banks.rel(bi)

                    def gen_oza(l=l):
                        for half in range(2):
                            wo = next_w("in", l, C_O + half * 512)
                            for m in range(4):
                                feat_proj(wo, m, lambda ps, m=m: act(SG3[:, m, :], ps[:], AF.Sigmoid))
                                yield
                            wz = next_w("in", l, C_ZA + half * 512)
                            for m in range(4):
                                j = half * 4 + m
                                sz = tf()
                                feat_proj(wz, m, lambda ps, sz=sz: act(sz[:], ps[:], AF.Silu))
                                tt(V, GA[:, j, :], SG3[:, m, :], sz[:], ALU.mult)
                                yield
                    def gen_pb(l=l):
                        for half in range(2):
                            wpb = next_w("pb", l, half * 512)
                            for m in range(4):
                                j = half * 4 + m
                                bi, ps = pbank()
                                for kc in range(8):
                                    mm(ps[:], wpb[:, kc, m * 128:(m + 1) * 128], UB[:, kc, :], kc == 0, kc == 7)
                                act(YB[:, j, :], ps[:], AF.Copy)
                                banks.rel(bi)
                                yield

                    def gen_fill():
                        yield from gen_oza()
                        yield from gen_pb()
                    fill = gen_fill()

                    def filler(n):
                        for _ in range(n):
                            next(fill, None)

                    S.mark(f"{ti}.{l}.mlstm")
                    def stage1a(b, l=l):
                        blk = slice(b * 128, (b + 1) * 128)
                        bs, pS = pbank()
                        for h in range(4):
                            for dc in range(2):
                                mm(pS[:, h * 128:(h + 1) * 128], Kf[:, 2 * h + dc, blk], Qf[:, 2 * h + dc, blk], dc == 0, dc == 1)
                        s0 = S0T[b % 2]
                        for h in range(4):
                            gi = b * 4 + h
                            stt(V, s0[:, h, :], pS[:, h * 128:(h + 1) * 128], csc[:, gi:gi + 1], tri, ALU.mult, ALU.mult)
                        banks.rel(bs)

                    def stage1b(b, l=l):
                        blk = slice(b * 128, (b + 1) * 128)
                        s0 = S0T[b % 2]
                        pNs = []
                        for h in range(4):
                            bn, pN = pbank()
                            pNs.append((bn, pN))
                            mm(pN[:, 0:257], s0[:, h, :], Vt[:, b, h, :], True, False)
                            for dc in range(2):
                                mm(pN[:, 0:257], Qf[:, 2 * h + dc, blk], Cbf[l][:, dc, h, :], False, dc == 1)
                        for h in range(4):
                            gi = b * 4 + h
                            for dc in range(2):
                                bc, pC = pbank()
                                mm(pC[:, 0:257], KPP[:, b, h * 256 + dc * 128:h * 256 + (dc + 1) * 128], Vt[:, b, h, :], True, True)
                                stt(V, Cst[l][:, dc, h, :], Cst[l][:, dc, h, :], ebt[:, gi:gi + 1], pC[:, 0:257], ALU.mult, ALU.add)
                                banks.rel(bc)
                                act(Cbf[l][:, dc, h, :], Cst[l][:, dc, h, :], AF.Copy)
                        return pNs

                    def stage2(b, pNs):
                        for h in range(4):
                            gi = b * 4 + h
                            act(dn[:, h:h + 1], pNs[h][1][:, 256:257], AF.Abs, scale=r16[:, gi:gi + 1])
                        ts(V, dn, dn, 1.0, ALU.max)
                        S.op(V, lambda e: e.reciprocal(out=rec, in_=dn), [dn], [rec])
                        tt(V, hs, r16[:, b * 4:(b + 1) * 4], rec, ALU.mult)
                        h1 = H1[b % 2]
                        for h in range(4):
                            bn, pN = pNs[h]
                            act(junk[:, 0:256], pN[:, 0:256], AF.Square, scale=hs[:, h:h + 1], accum_out=ssq2[:, h:h + 1])
                            act(h1[:, h, :], pN[:, 0:256], AF.Copy, scale=hs[:, h:h + 1])
                            banks.rel(bn)

                    def stage3a(b):
                        act(lnv2, ssq2, AF.Ln, bias=EPS, scale=1.0 / 256.0)
                        act(rs2, lnv2, AF.Exp, scale=-0.5)
                        h1 = H1[b % 2]
                        hm = HM[b % 2]
                        for h in range(4):
                            ts(V, hm[:, h, :], h1[:, h, :], rs2[:, h:h + 1], ALU.mult)

                    def stage3b(b, l=l):
                        blk = slice(b * 128, (b + 1) * 128)
                        hm = HM[b % 2]
                        bt, pT = pbank()
                        ptt = pT[:].bitcast(BF16)
                        for h in range(4):
                            for ec in range(2):
                                c = 2 * h + ec
                                tr(ptt[:, c * 128:(c + 1) * 128], hm[:, h, ec * 128:(ec + 1) * 128], signal=(c == 7))
                        for c in range(8):
                            if b % 2 == 0:
                                ts(V, OA[:, c, blk], ptt[:, c * 128:(c + 1) * 128], lc(l, LC_MG + c, 1), ALU.mult)
                            else:
                                act(OA[:, c, blk], ptt[:, c * 128:(c + 1) * 128], AF.Copy, scale=lc(l, LC_MG + c, 1))
                        banks.rel(bt)

                    stage1a(0)
                    for b in range(NB):
                        pNs = stage1b(b)
                        if b > 0:
                            stage3a(b - 1)
                        filler(3)
                        if b > 0:
                            stage3b(b - 1)
                        if b + 1 < NB:
                            stage1a(b + 1)
                        stage2(b, pNs)
                        filler(2 if b < NB - 1 else 1)
                    stage3a(NB - 1)
                    filler(2)
                    stage3b(NB - 1)
                    for _ in fill:
                        pass
                    dbg("Vt", Vt[:]); dbg("small", small[:]); dbg("GA", GA[:])
                    for c in range(8):
                        tt(V, OA[:, c, :], OA[:, c, :], GA[:, c, :], ALU.mult)

                    dbg("KPP", KPP[:]); dbg("OA", OA[:]); dbg("Cst", Cst[l][:])
                    S.mark(f"{ti}.{l}.merge")
                    for half in range(2):
                        wga = next_w("in", l, C_GA + half * 512)
                        for m in range(4):
                            feat_proj(wga, m, lambda ps, m=m: act(SGA[:, m, :], ps[:], AF.Sigmoid))
                        wpa = next_w("pa", l, half * 512)
                        for m in range(4):
                            bi, ps = pbank()
                            for kc in range(8):
                                mm(ps[:], wpa[:, kc, m * 128:(m + 1) * 128], OA[:, kc, :], kc == 0, kc == 7)
                            tt(V, SGA[:, m, :], ps[:], SGA[:, m, :], ALU.mult)
                            banks.rel(bi)
                        wgb = next_w("in", l, C_GB + half * 512)
                        for m in range(4):
                            feat_proj(wgb, m, lambda ps, m=m: act(SGB[:, m, :], ps[:], AF.Sigmoid))
                        for m in range(4):
                            j = half * 4 + m
                            tt(V, SGB[:, m, :], YB[:, j, :], SGB[:, m, :], ALU.mult)
                            tt(P, MG[:, j, :], SGA[:, m, :], SGB[:, m, :], ALU.add)
                    dbg("MG", MG[:])
                    S.mark(f"{ti}.{l}.outproj")
                    for half in range(2):
                        w = next_w("out", l, half * 512)
                        for b in range(NB):
                            bi, ps = pbank()
                            for kc in range(8):
                                mm(ps[:], MG[:, kc, b * 128:(b + 1) * 128], w[:, kc, :], kc == 0, kc == 7)
                            xs = x[:, b, half * 512:(half + 1) * 512]
                            tt(V, xs, ps[:], xs, ALU.add)
                            banks.rel(bi)

                    dbg("x1", x[:])
                S.mark(f"{ti}.fin")
                rms_stats(x)
                for b in range(NB):
                    stt(V, OUT[:, b, :], x[:, b, :], rstd[:, b:b + 1], fg, ALU.mult, ALU.mult)
                r0 = (s * n_tiles + t) * TT
                S.dma(P, f"o{ti % 2}", y_d[r0:r0 + TT, :].rearrange("(b p) d -> p b d", p=128), OUT[:], track_out=False)


        S.dry = True
        main_loop()
        S.dry = False
        banks.reset(); tmpf_i[0] = 0; tmpb_i[0] = 0; dbg_done.clear()
        wstate["loaded"] = 0; wstate["use"] = 0; wstate.pop("cast1_waited", None)
        S.wait_all("sync", ["cast0"] + [f"w{i}" for i in range(NW)])
        emit_xload(0)
        main_loop()

        S.wait_all(P, ["o0", "o1"])
        assert wstate["use"] == len(wlist)

        with nc.Block() as block:
            @block.tensor
            def _(e):
                for f in S.q["tensor"]:
                    f(e)

            @block.scalar
            def _(e):
                for f in S.q["scalar"]:
                    f(e)

            @block.vector
            def _(e):
                for f in S.q["vector"]:
                    f(e)

            @block.gpsimd
            def _(e):
                for f in S.q["gpsimd"]:
                    f(e)

            @block.sync
            def _(e):
                for f in S.q["sync"]:
                    f(e)
        build_program.last_ninst = S.ninst
        build_program.marks = S.marks
    return nc


def make_cpack(norm_g, b_if, conv_qk_w, conv_qk_b, mhn_g, dw_w, dw_b, ln_g, ln_b, w_in, final_g):
    cpk = np.zeros((128, NCONST), np.float32)

    def pc(v):
        return np.ascontiguousarray(v.reshape(-1, 128).T)

    for l in range(L):
        o = l * LC_SIZE
        cpk[:, o + LC_NG:o + LC_NG + 8] = pc(norm_g[l])
        cpk[:, o + LC_QKW:o + LC_QKW + 64] = conv_qk_w[l].reshape(4, 16, 128).transpose(2, 1, 0).reshape(128, 64)
        cpk[:, o + LC_QKB:o + LC_QKB + 16] = pc(conv_qk_b[l])
        cpk[:, o + LC_MG:o + LC_MG + 8] = pc(mhn_g[l])
        wpad = np.concatenate([dw_w[l], np.zeros((1, 1024), np.float32)], axis=0)
        wr = wpad.reshape(8, 4, 8, 4, 32)
        cpk[:, o + LC_DWW:o + LC_DWW + 256] = wr.transpose(1, 4, 2, 0, 3).reshape(128, 256)
        cpk[:, o + LC_DWB:o + LC_DWB + 8] = pc(dw_b[l])
        cpk[:, o + LC_LNG:o + LC_LNG + 8] = pc(ln_g[l])
        cpk[:, o + LC_LNB:o + LC_LNB + 8] = pc(ln_b[l])
        cpk[:, o + LC_BIF:o + LC_BIF + 32] = np.tile(b_if[l][None, :], (128, 4))
        cpk[:, o + LC_WIF:o + LC_WIF + 64] = w_in[l][:, C_IF:C_IF + 8].reshape(8, 128, 8).transpose(1, 0, 2).reshape(128, 64)
    cpk[:, GC_FG:GC_FG + 1024] = np.tile(final_g[None, :], (128, 1))
    cpk[:, GC_TRI:GC_TRI + 128] = np.triu(np.ones((128, 128), np.float32))
    cpk[:, GC_ID:GC_ID + 128] = np.eye(128, dtype=np.float32)
    cpk[:, GC_IDR:GC_IDR + 32] = np.tile(np.eye(32, dtype=np.float32), (4, 1))
    return cpk


def kernel(x, norm_g, w_in, b_if, conv_qk_w, conv_qk_b, mhn_g, dw_w, dw_b, ln_g, ln_b,
           w_pa, w_pb, w_out, final_g, _n_cores=8):
    x = np.asarray(x, np.float32)
    B, Sq, Dm = x.shape
    n_cores = _n_cores
    spc = B // n_cores
    n_tiles = Sq // TT
    f = lambda a: np.ascontiguousarray(np.asarray(a, np.float32))
    w_in, w_pa, w_pb, w_out = f(w_in), f(w_pa), f(w_pb), f(w_out)
    cpk = make_cpack(f(norm_g), f(b_if), f(conv_qk_w), f(conv_qk_b), f(mhn_g), f(dw_w), f(dw_b),
                     f(ln_g), f(ln_b), w_in, f(final_g))
    nc = build_program(spc, n_tiles)
    in_maps = []
    for c in range(n_cores):
        xs = np.ascontiguousarray(x[c * spc:(c + 1) * spc].reshape(spc * Sq, Dm))
        in_maps.append({"x": xs, "w_in": w_in, "w_pa": w_pa, "w_pb": w_pb, "w_out": w_out, "cpack": cpk})
    res = run_bass_kernel_spmd(nc, in_maps, core_ids=list(range(n_cores)))
    out = np.concatenate([np.asarray(r["y"]).reshape(spc, Sq, Dm) for r in res.results], axis=0)
    return out.astype(np.float32)
```
